# Optimizing a Trainium2 kernel written in Bass

```python
import jax, jax.numpy as jnp
from jax import lax
import numpy as np

D_MODEL = 2048
BATCH = 2
SEQ = 16384
DEPTH = 2
DEC_BATCH = 8
DEC_SEQ = 64
PAST_LEN = 1024

CHUNK = 64
GMLP_CHUNK = 128
D_MIX = D_MODEL
D_A = D_MIX // 2
D_B = D_MIX - D_A
H_A = 8
HD_A = D_A // H_A
H_B = 8
CONV_K = 3
D_FF = 5632
D_IN = 2 * D_A + 3 * D_B
EPS = 1e-6

kernel_name = "hybrid_gmlp_shortconv_streaming_step"


def _rmsnorm(x, g):
    xf = x.astype(jnp.float32)
    y = xf * lax.rsqrt(jnp.mean(xf * xf, axis=-1, keepdims=True) + EPS)
    return (y * g.astype(jnp.float32)).astype(x.dtype)


def _causal_dwconv(x, prev, w):
    t = x.shape[1]
    xp = jnp.concatenate([prev.astype(x.dtype), x], axis=1)
    w = w.astype(x.dtype)
    y = w[0] * xp[:, 0:t]
    for k in range(1, CONV_K):
        y = y + w[k] * xp[:, k:k + t]
    return y, xp[:, -(CONV_K - 1):]


def _gmlp_spatial(u, v, ws, bs):
    b, t, _ = v.shape
    n = -(-t // GMLP_CHUNK)
    pad = n * GMLP_CHUNK - t
    vp = jnp.pad(v, ((0, 0), (0, pad), (0, 0))).reshape(b, n, GMLP_CHUNK, H_A, HD_A)
    blk = jnp.arange(GMLP_CHUNK) // CHUNK
    mask = blk[:, None] >= blk[None, :]
    wm = jnp.where(mask[None], ws, 0.0).astype(v.dtype)
    z = jnp.einsum('hts,bnshd->bnthd', wm, vp)
    z = z + jnp.transpose(bs).astype(v.dtype)[None, None, :, :, None]
    z = z.reshape(b, n * GMLP_CHUNK, D_A)[:, :t]
    return u * z


def _layer(x, sconv_prev, ffn_prev, n1_g, w_in, vnorm_g, ws, bs, sconv_w,
           onorm_a_g, onorm_b_g, w_out, n2_g, ffn_up, ffn_conv_w, ffn_down):
    bsz, t, _ = x.shape
    h = _rmsnorm(x, n1_g)
    proj = h @ w_in.astype(x.dtype)
    u, v, gb, gc, hin = jnp.split(
        proj, [D_A, 2 * D_A, 2 * D_A + D_B, 2 * D_A + 2 * D_B], axis=-1)
    u = jax.nn.gelu(u)
    v = _rmsnorm(jax.nn.gelu(v).reshape(bsz, t, H_A, HD_A),
                 vnorm_g.reshape(H_A, HD_A)).reshape(bsz, t, D_A)
    a = _gmlp_spatial(u, v, ws, bs)
    cb, sconv_new = _causal_dwconv(gc * hin, sconv_prev, sconv_w)
    bout = gb * cb
    mix = jnp.concatenate([_rmsnorm(a, onorm_a_g), _rmsnorm(bout, onorm_b_g)], axis=-1)
    x = x + mix @ w_out.astype(x.dtype)
    h2 = _rmsnorm(x, n2_g)
    up, ffn_new = _causal_dwconv(h2 @ ffn_up.astype(x.dtype), ffn_prev, ffn_conv_w)
    g, val = jnp.split(up, 2, axis=-1)
    x = x + (jax.nn.silu(g) * val) @ ffn_down.astype(x.dtype)
    return x, v, sconv_new, ffn_new


def setup_inputs(seed: int = 0) -> dict:
    key = jax.random.key(seed)
    ks = jax.random.split(key, 20)
    nrm = jax.random.normal
    f32 = jnp.float32
    return {
        "x_prompt": nrm(ks[0], (BATCH, SEQ, D_MODEL), f32),
        "x_sample": nrm(ks[1], (DEC_BATCH, DEC_SEQ, D_MODEL), f32),
        "state_sconv": nrm(ks[2], (DEPTH, DEC_BATCH, CONV_K - 1, D_B), f32),
        "state_ffnconv": nrm(ks[3], (DEPTH, DEC_BATCH, CONV_K - 1, 2 * D_FF), f32),
        "n1_g": 1.0 + 0.02 * nrm(ks[4], (DEPTH, D_MODEL), f32),
        "w_in": nrm(ks[5], (DEPTH, D_MODEL, D_IN), f32) * D_MODEL ** -0.5,
        "vnorm_g": 1.0 + 0.02 * nrm(ks[6], (DEPTH, D_A), f32),
        "gmlp_ws": nrm(ks[7], (DEPTH, H_A, GMLP_CHUNK, GMLP_CHUNK), f32) * GMLP_CHUNK ** -0.5,
        "gmlp_bs": 1.0 + 0.02 * nrm(ks[8], (DEPTH, H_A, GMLP_CHUNK), f32),
        "sconv_w": nrm(ks[9], (DEPTH, CONV_K, D_B), f32) * CONV_K ** -0.5,
        "onorm_a_g": 1.0 + 0.02 * nrm(ks[10], (DEPTH, D_A), f32),
        "onorm_b_g": 1.0 + 0.02 * nrm(ks[11], (DEPTH, D_B), f32),
        "w_out": nrm(ks[12], (DEPTH, D_MIX, D_MODEL), f32) * D_MIX ** -0.5,
        "n2_g": 1.0 + 0.02 * nrm(ks[13], (DEPTH, D_MODEL), f32),
        "ffn_up": nrm(ks[14], (DEPTH, D_MODEL, 2 * D_FF), f32) * D_MODEL ** -0.5,
        "ffn_conv_w": nrm(ks[15], (DEPTH, CONV_K, 2 * D_FF), f32) * CONV_K ** -0.5,
        "ffn_down": nrm(ks[16], (DEPTH, D_FF, D_MODEL), f32) * D_FF ** -0.5,
        "final_g": 1.0 + 0.02 * nrm(ks[17], (D_MODEL,), f32),
    }


def reference(x_prompt, x_sample, state_sconv, state_ffnconv, n1_g, w_in, vnorm_g,
              gmlp_ws, gmlp_bs, sconv_w, onorm_a_g, onorm_b_g, w_out, n2_g,
              ffn_up, ffn_conv_w, ffn_down, final_g):
    xp = x_prompt
    xs = x_sample
    p_sconv, p_ffn, s_v, s_sconv, s_ffn = [], [], [], [], []
    for l in range(DEPTH):
        w = (n1_g[l], w_in[l], vnorm_g[l], gmlp_ws[l], gmlp_bs[l], sconv_w[l],
             onorm_a_g[l], onorm_b_g[l], w_out[l], n2_g[l], ffn_up[l],
             ffn_conv_w[l], ffn_down[l])
        zs = jnp.zeros((xp.shape[0], CONV_K - 1, D_B), xp.dtype)
        zf = jnp.zeros((xp.shape[0], CONV_K - 1, 2 * D_FF), xp.dtype)
        xp, _, ps, pf = _layer(xp, zs, zf, *w)
        p_sconv.append(ps)
        p_ffn.append(pf)
        xs, sv, ss, sf = _layer(xs, state_sconv[l], state_ffnconv[l], *w)
        s_v.append(sv)
        s_sconv.append(ss)
        s_ffn.append(sf)
    y_prompt = _rmsnorm(xp, final_g)
    y_sample = _rmsnorm(xs, final_g)
    return (y_prompt, y_sample, jnp.stack(p_sconv), jnp.stack(p_ffn),
            jnp.stack(s_v), jnp.stack(s_sconv), jnp.stack(s_ffn))
```

```python
from contextlib import ExitStack

import numpy as np
import concourse.bass as bass
import concourse.mybir as mybir
from concourse.bass_utils import run_bass_kernel_spmd

F32 = mybir.dt.float32
BF16 = mybir.dt.bfloat16
AF = mybir.ActivationFunctionType
ALU = mybir.AluOpType
AX = mybir.AxisListType

D = 2048
KC = 16
DA = 1024
DB = 1024
DIN = 5120
DFF = 5632
FC = 44
FUP = 11264
L = 2
EPS = 1e-6
NCORES = 8
SEQ = 16384
HALO = 2
JIT_CONVERSION = True

ENGS = ("pe", "act", "dve", "pool", "sp")
SAME_ENGINE_SYNC = True


class DmaSem:
    def __init__(self, handle):
        self.handle = handle
        self.count = 0


class Op:
    __slots__ = ("eng", "name", "kw", "waits", "signal", "idx", "evnum", "dsem", "dval", "uid")


class Sched:
    def __init__(self):
        self.streams = {e: [] for e in ENGS}
        self.last_writer = {}
        self.readers = {}
        self.waited = {e: {} for e in ENGS}
        self.uid = 0

    def add(self, eng, name, kw, reads=(), writes=(), dsem=None, after=()):
        op = Op()
        op.eng = eng
        op.name = name
        op.kw = kw
        op.waits = []
        op.signal = False
        op.idx = len(self.streams[eng])
        op.evnum = None
        op.dsem = dsem
        op.uid = self.uid
        self.uid += 1
        if dsem is not None:
            dsem.count += 16
            op.dval = dsem.count
        else:
            op.dval = None
        deps = {}
        for r in list(reads) + list(after):
            w = self.last_writer.get(r)
            if w is not None:
                deps[w.uid] = w
        for r in writes:
            w = self.last_writer.get(r)
            if w is not None:
                deps[w.uid] = w
            for rd in self.readers.get(r, {}).values():
                deps[rd.uid] = rd
        best = {}
        for d in deps.values():
            k = ("sem", id(d.dsem)) if d.dsem is not None else ("eng", d.eng)
            v = d.dval if d.dsem is not None else d.idx
            if k not in best or v > best[k][0]:
                best[k] = (v, d)
        for _, d in best.values():
            self._dep(op, d)
        for r in writes:
            self.last_writer[r] = op
            self.readers[r] = {}
        key = op.eng if dsem is None else ("dma", op.uid)
        for r in reads:
            self.readers.setdefault(r, {})[key] = op
        self.streams[eng].append(op)
        return op

    def _dep(self, op, d):
        wd = self.waited[op.eng]
        if d.dsem is not None:
            k = ("sem", id(d.dsem))
            if wd.get(k, -1) >= d.dval:
                return
            wd[k] = d.dval
            op.waits.append(d)
            return
        if d.eng == op.eng and op.dsem is None:
            if d.eng == "pe" or not SAME_ENGINE_SYNC:
                return
        k = ("eng", d.eng)
        if wd.get(k, -1) >= d.idx:
            return
        wd[k] = d.idx
        d.signal = True
        op.waits.append(d)

    def emit(self, block, esem, final_waits=()):
        for e in ENGS:
            n = 0
            for op in self.streams[e]:
                if op.dsem is None and op.signal:
                    n += 1
                    op.evnum = n

        def run(e, eng):
            for op in self.streams[e]:
                for d in op.waits:
                    if d.dsem is not None:
                        eng.wait_ge(d.dsem.handle, d.dval)
                    else:
                        eng.wait_ge(esem[d.eng], d.evnum)
                ins = getattr(eng, op.name)(**op.kw)
                if op.dsem is not None:
                    ins.then_inc(op.dsem.handle, 16)
                elif op.signal:
                    ins.then_inc(esem[e], 1)
            if e == "sp":
                for (h, v) in final_waits:
                    eng.wait_ge(h, v)

        @block.tensor
        def _(eng):
            run("pe", eng)

        @block.scalar
        def _(eng):
            run("act", eng)

        @block.vector
        def _(eng):
            run("dve", eng)

        @block.gpsimd
        def _(eng):
            run("pool", eng)

        @block.sync
        def _(eng):
            run("sp", eng)

    def stats(self):
        return {e: (len(s), sum(len(o.waits) for o in s)) for e, s in self.streams.items()}


def make_tiles(npc):
    tiles = []
    nfull, rem = divmod(npc, 4)
    for t in range(nfull):
        tiles.append([("p", 4 * t + i, 128) for i in range(4)])
    tiles.append([("p", 4 * nfull + i, 128) for i in range(rem)] + [("s", 0, 64)])
    return tiles


def build(npc):
    nc = bass.Bass("TRN2", target_bir_lowering=False)
    npt = npc * 128
    nown = (npc - HALO) * 128

    def di(n, s):
        return nc.dram_tensor(n, s, F32, kind="ExternalInput").ap()

    def do(n, s):
        return nc.dram_tensor(n, s, F32, kind="ExternalOutput").ap()

    xp = di("xp", [npt, D])
    xs = di("xs", [64, D])
    st_s = di("st_s", [L, 2, DB])
    st_f = di("st_f", [L, 2, FUP])
    n1_g = di("n1_g", [L, D])
    w_in = di("w_in", [L, D, DIN])
    vnorm_g = di("vnorm_g", [L, DA])
    gmlp_ws = di("gmlp_ws", [L, 8, 128, 128])
    gmlp_bs = di("gmlp_bs", [1, L * 1024])
    sconv_w = di("sconv_w", [L, 3, DB])
    onorm_a_g = di("onorm_a_g", [L, DA])
    onorm_b_g = di("onorm_b_g", [L, DB])
    w_out = di("w_out", [L, D, D])
    n2_g = di("n2_g", [L, D])
    ffn_up = di("ffn_up", [L, D, FUP])
    ffn_conv_w = di("ffn_conv_w", [L, 3, FUP])
    ffn_down = di("ffn_down", [L, DFF, D])
    final_g = di("final_g", [1, D])

    yp = do("yp", [nown, D])
    ys = do("ys", [64, D])
    p_sconv = do("p_sconv", [L, 2, DB])
    p_ffn = do("p_ffn", [L, 2, FUP])
    s_v = do("s_v", [L, 64, DA])
    s_sconv = do("s_sconv", [L, 2, DB])
    s_ffn = do("s_ffn", [L, 2, FUP])

    wb_in = nc.dram_tensor("wb_in", [L, D, DIN], BF16, kind="Internal").ap()
    wb_out = nc.dram_tensor("wb_out", [L, D, D], BF16, kind="Internal").ap()
    wb_up = nc.dram_tensor("wb_up", [L, D, FUP], BF16, kind="Internal").ap()
    wb_down = nc.dram_tensor("wb_down", [L, DFF, D], BF16, kind="Internal").ap()

    tiles = make_tiles(npc)

    with ExitStack() as es:
        def sb(n, s, d):
            return es.enter_context(nc.sbuf_tensor(n, s, d))

        x_tok = sb("x_tok", [128, 4, D], F32)
        hT = sb("hT", [128, KC, 512], BF16)
        big = sb("big", [128, 44 * 512], BF16)
        ring = sb("ring", [128, 4, 8192], BF16)
        gbc = sb("gbc", [128, D], F32)
        NTMP = 10
        tmps = [sb(f"tmp{i}", [128, 516], F32) for i in range(NTMP)]
        NSQ = 3
        sqs = [sb(f"sq{i}", [128, 512], BF16) for i in range(NSQ)]
        bhi = sb("bhi", [1, L * 1024], BF16)
        blo = sb("blo", [1, L * 1024], BF16)
        WsT = sb("WsT", [128, L, 1024], BF16)
        ident_bf = sb("ident_bf", [128, 128], BF16)
        ident_f = sb("ident_f", [128, 128], F32)
        ones_bf = sb("ones_bf", [128, 128], BF16)
        parA = sb("parA", [128, L, 56], F32)
        cwf = sb("cwf", [128, L, 3, 88], F32)
        hist_s = sb("hist_s", [128, L, 2, 16], F32)
        hist_f = sb("hist_f", [128, L, 2, 176], F32)
        nstat = sb("nstat", [128, 12], F32)
        vstat = sb("vstat", [128, 96], F32)
        rstat = sb("rstat", [128, 16], F32)
        mhalf = sb("mhalf", [128, 32], F32)
        outst = sb("outst", [128, 128], F32)

        banks = [es.enter_context(nc.psum_tensor(f"pb{i}", [128, 512], F32)) for i in range(8)]
        statbank = banks[7]

        NSEM = 40
        sems = [es.enter_context(nc.semaphore(f"sm{i}")) for i in range(NSEM)]
        esem = {e: sems[i] for i, e in enumerate(ENGS)}
        sem_pool = [DmaSem(h) for h in sems[5:]]
        sem_it = iter(sem_pool)

        def newsem():
            return next(sem_it)

        block = es.enter_context(nc.Block())
        S = Sched()

        def cells(a, b):
            return [("big", i) for i in range(a, b)]

        def bigv(c0, ncell, dtype=BF16):
            v = big[:, c0 * 512:(c0 + ncell) * 512]
            if dtype is F32:
                v = v.bitcast(F32)
            return v

        state = {"bank": 0, "tmp": 0, "sq": 0, "slot": 0}

        def next_bank():
            b = state["bank"]
            state["bank"] = (b + 1) % 7
            return b

        def next_tmp():
            t = state["tmp"]
            state["tmp"] = (t + 1) % NTMP
            return t

        def next_sq():
            t = state["sq"]
            state["sq"] = (t + 1) % NSQ
            return t

        ring_sems = [newsem() for _ in range(4)]

        def slab_load(src, shape3, reads, tag=None):
            s = state["slot"]
            state["slot"] = (s + 1) % 4
            k, n = shape3
            dst = ring[:, s, 0:k * n].rearrange("p (k n) -> p k n", k=k)
            S.add("sp", "dma_start", dict(out=dst, in_=src), reads=reads, writes=[("ring", s)], dsem=ring_sems[s])
            if tag is not None:
                pi, idx, nsl = tag
                if pi >= 1 and idx == max(0, nsl - 3):
                    issue_conversion(pi + 1, [("ring", s)])
            return dst, ("ring", s)

        evac_flip = {"i": 0}

        def evac_copy(out, in_, reads, writes):
            evac_flip["i"] ^= 1
            if evac_flip["i"]:
                S.add("act", "activation", dict(out=out, in_=in_, func=AF.Copy), reads=reads, writes=writes)
            else:
                S.add("dve", "tensor_copy", dict(out=out, in_=in_), reads=reads, writes=writes)

        conv_keys = {}

        CONV_PHASES = [(nm, l) for l in range(L) for nm in ("in", "out", "up", "down")]
        conv_src = {"in": (w_in, wb_in, D), "out": (w_out, wb_out, D), "up": (ffn_up, wb_up, D), "down": (ffn_down, wb_down, DFF)}
        conv_done = set()

        def issue_conversion(pi, after):
            if pi in conv_done or pi >= len(CONV_PHASES):
                return
            conv_done.add(pi)
            nm, l = CONV_PHASES[pi]
            src, dst, rows = conv_src[nm]
            sem = newsem()
            npieces = 8
            step = rows // npieces
            keys = []
            fr = list(after)
            for i in range(npieces):
                r0, r1 = i * step, (i + 1) * step if i < npieces - 1 else rows
                key = ("wb", nm, l, i)
                S.add("pool", "dma_start", dict(out=dst[l, r0:r1, :], in_=src[l, r0:r1, :]), after=fr, writes=[key], dsem=sem)
                fr = []
                keys.append(key)
            conv_keys[(nm, l)] = keys

        def issue_conversions(first_reads):
            issue_conversion(0, first_reads)
            if not JIT_CONVERSION:
                for pi in range(1, len(CONV_PHASES)):
                    issue_conversion(pi, [])
            else:
                issue_conversion(1, [])

        def conv_tick(flush=False):
            pass

        S.add("pool", "memset", dict(ap=ident_bf[:], constant=0.0), writes=["ident_bf"])
        S.add("pool", "affine_select", dict(out=ident_bf[:], in_=ident_bf[:], pattern=[[-1, 128]],
                                            compare_op=ALU.not_equal, fill=1.0, base=0, channel_multiplier=1),
              reads=["ident_bf"], writes=["ident_bf"])
        S.add("pool", "memset", dict(ap=ident_f[:], constant=0.0), writes=["ident_f"])
        S.add("pool", "affine_select", dict(out=ident_f[:], in_=ident_f[:], pattern=[[-1, 128]],
                                            compare_op=ALU.not_equal, fill=1.0, base=0, channel_multiplier=1),
              reads=["ident_f"], writes=["ident_f"])
        S.add("pool", "memset", dict(ap=ones_bf[:], constant=1.0), writes=["ones_bf"])
        S.add("pool", "memset", dict(ap=mhalf[:], constant=-0.5), writes=["mhalf"])
        S.add("pool", "memset", dict(ap=nstat[:], constant=1.0), writes=["nstat"])
        S.add("pool", "memset", dict(ap=vstat[:], constant=1.0), writes=["vstat"])
        S.add("pool", "memset", dict(ap=rstat[:], constant=1.0), writes=["rstat"])
        S.add("pool", "memset", dict(ap=hist_s[:], constant=0.0), writes=["hist_s_all"])
        S.add("pool", "memset", dict(ap=hist_f[:], constant=0.0), writes=["hist_f_all"])

        stgs = [bigv(3 * l, 3, F32).rearrange("p (s c) -> p s c", s=6) for l in range(L)]
        wtmps = [bigv(6 + 4 * l, 4, F32).rearrange("p (h s) -> p h s", h=8) for l in range(L)]
        wtmp_bfs = [bigv(14 + 2 * l, 2).rearrange("p (h s) -> p h s", h=8) for l in range(L)]
        bsb = bigv(18, 8, F32)
        bsem = newsem()
        stg_sems = [newsem() for _ in range(L)]
        ws_sems = [newsem() for _ in range(L)]
        early_keys = []

        S.add("act", "dma_start", dict(out=bsb[0:1, :], in_=gmlp_bs[0:1, :]), writes=["bsb_ld"], dsem=bsem)
        early_keys.append("bsb_ld")
        stg_keys = {}
        for l in range(L):
            stg = stgs[l]
            ks = []

            def sld(out, in_, key):
                S.add("act", "dma_start", dict(out=out, in_=in_), writes=[key], dsem=stg_sems[l])
                ks.append(key)

            sld(stg[0:8, 0, :], onorm_a_g[l].rearrange("(j p) -> j p", p=128), ("stg", l, "a"))
            sld(stg[8:16, 0, :], onorm_b_g[l].rearrange("(j p) -> j p", p=128), ("stg", l, "b"))
            sld(stg[16:40, 0, :], sconv_w[l].rearrange("k (j p) -> (k j) p", p=128), ("stg", l, "c"))
            sld(stg[40:56, 0, :], st_s[l].rearrange("r (j p) -> (r j) p", p=128), ("stg", l, "d"))
            for k in range(3):
                sld(stg[0:88, 1 + k, :], ffn_conv_w[l, k].rearrange("(j p) -> j p", p=128), ("stg", l, "w", k))
            for r in range(2):
                sld(stg[0:88, 4 + r, :], st_f[l, r].rearrange("(j p) -> j p", p=128), ("stg", l, "f", r))
            stg_keys[l] = ks
            early_keys += ks
            S.add("act", "dma_start", dict(out=wtmps[l], in_=gmlp_ws[l].rearrange("h t s -> t h s")),
                  writes=[("wtmp", l)], dsem=ws_sems[l])
            early_keys.append(("wtmp", l))

        xsems = [newsem() for _ in range(4)]
        gsem = newsem()
        for c, (kind, ci, n_) in enumerate(tiles[0]):
            src = xp[ci * 128:(ci + 1) * 128, :] if kind == "p" else xs[:, :]
            S.add("act", "dma_start", dict(out=x_tok[0:n_, c, :], in_=src), writes=[("x", c, g) for g in range(4)], dsem=xsems[c])
            early_keys += [("x", c, g) for g in range(4)]
        S.add("act", "dma_start", dict(out=gbc[:], in_=n1_g[0:1, :].partition_broadcast(128)), writes=["gbc"], dsem=gsem)
        early_keys.append("gbc")
        pre = {"gbc": True}

        issue_conversions(early_keys)

        S.add("dve", "tensor_copy", dict(out=bhi[0:1, :], in_=bsb[0:1, :]), reads=["bsb_ld"], writes=["bhi"])
        S.add("dve", "tensor_tensor", dict(out=bsb[0:1, :], in0=bsb[0:1, :], in1=bhi[0:1, :], op=ALU.subtract),
              reads=["bsb_ld", "bhi"], writes=["bsb_ld"])
        S.add("dve", "tensor_copy", dict(out=blo[0:1, :], in_=bsb[0:1, :]), reads=["bsb_ld"], writes=["blo"])

        for l in range(L):
            stg = stgs[l]
            wtmp = wtmps[l]
            wtmp_bf = wtmp_bfs[l]
            sk = stg_keys[l]
            b = next_bank()
            S.add("pe", "transpose", dict(out=banks[b][:, 0:56], in_=stg[0:56, 0, :], identity=ident_f[0:56, 0:56]),
                  reads=sk + ["ident_f"], writes=[("bank", b)])
            S.add("dve", "tensor_copy", dict(out=parA[:, l, :], in_=banks[b][:, 0:56]), reads=[("bank", b)], writes=["parA"])
            S.add("dve", "tensor_copy", dict(out=hist_s[:, l, 1, :], in_=banks[b][:, 40:56]), reads=[("bank", b), "hist_s_all"],
                  writes=[("hs", l, 1)])
            for k in range(3):
                b = next_bank()
                S.add("pe", "transpose", dict(out=banks[b][:, 0:88], in_=stg[0:88, 1 + k, :], identity=ident_f[0:88, 0:88]),
                      reads=sk + ["ident_f"], writes=[("bank", b)])
                S.add("dve", "tensor_copy", dict(out=cwf[:, l, k, :], in_=banks[b][:, 0:88]), reads=[("bank", b)], writes=["cwf"])
            for r in range(2):
                b = next_bank()
                S.add("pe", "transpose", dict(out=banks[b][:, 0:88], in_=stg[0:88, 4 + r, :], identity=ident_f[0:88, 0:88]),
                      reads=sk + ["ident_f"], writes=[("bank", b)])
                S.add("dve", "tensor_copy", dict(out=hist_f[:, l, 1, r * 88:(r + 1) * 88], in_=banks[b][:, 0:88]),
                      reads=[("bank", b), "hist_f_all"], writes=[("hfinit", l, r)])
            S.add("pool", "memset", dict(ap=wtmp[0:64, :, 64:128], constant=0.0), reads=[("wtmp", l)], writes=[("wtmp", l)])
            S.add("dve", "tensor_copy", dict(out=wtmp_bf, in_=wtmp), reads=[("wtmp", l)], writes=[("wtmpbf", l)])
            b = next_bank()
            bbf = banks[b][:].bitcast(BF16)
            for h in range(8):
                S.add("pe", "transpose", dict(out=bbf[:, h * 128:(h + 1) * 128], in_=wtmp_bf[:, h, :], identity=ident_bf[:]),
                      reads=[("wtmpbf", l), "ident_bf"], writes=[("bank", b)])
            S.add("dve", "tensor_copy", dict(out=WsT[:, l, :], in_=bbf[:, :]), reads=[("bank", b)], writes=["WsT"])
        S.add("dve", "tensor_copy", dict(out=nstat[:, 8:9], in_=nstat[:, 9:10]),
              reads=["nstat", "parA", "cwf", "WsT", "bhi", "blo"] + [("hfinit", l, r) for l in range(L) for r in range(2)] + [("hs", l, 1) for l in range(L)],
              writes=cells(0, 26))

        ysems = [newsem() for _ in range(4)]
        vgsem = newsem()
        svsem = newsem()
        stsem = newsem()

        def xkeys(c):
            return [("x", c, g) for g in range(4)]

        def load_gbc(src_row):
            if pre["gbc"]:
                pre["gbc"] = False
                return
            S.add("act", "dma_start", dict(out=gbc[:], in_=src_row.partition_broadcast(128)), writes=["gbc"], dsem=gsem)

        def geom(tile):
            cs = [t[2] for t in tile]
            npr = sum(c for (k, _, c) in tile if k == "p")
            has_s = any(k == "s" for (k, _, _) in tile)
            gap = 2 if (npr and has_s) else 0
            co = []
            pos = 0
            for (k, _, c) in tile:
                if k == "s":
                    pos += gap
                co.append(pos)
                pos += c
            N = pos
            segs = []
            if npr:
                segs.append((0, npr, 0))
            if has_s:
                segs.append((npr + gap, 64, 1))
            return cs, co, N, segs

        def rstd_small(src, dst, n, scale, reads, writes):
            S.add("pool", "tensor_scalar", dict(out=dst, in0=src, scalar1=scale, scalar2=EPS, op0=ALU.mult, op1=ALU.add),
                  reads=reads, writes=writes)
            S.add("pool", "tensor_tensor", dict(out=dst, in0=dst, in1=mhalf[:, 0:n], op=ALU.pow),
                  reads=writes + ["mhalf"], writes=writes)

        def norm_to_hT(tile, gsrc_row):
            cs, co, N, segs = geom(tile)
            nch = len(cs)
            load_gbc(gsrc_row)
            junk = bigv(16, 4)
            for c in range(nch):
                S.add("act", "activation", dict(out=junk[0:cs[c], :], in_=x_tok[0:cs[c], c, :], func=AF.Square,
                                                accum_out=nstat[0:cs[c], c:c + 1]),
                      reads=xkeys(c), writes=cells(16, 20) + [("nstat", c)])
                rstd_small(nstat[:, c:c + 1], nstat[:, 4 + c:5 + c], 1, 1.0 / D,
                           reads=[("nstat", c)], writes=[("nrstd", c)])
                hn = bigv(4 * c, 4)
                S.add("dve", "scalar_tensor_tensor", dict(out=hn[0:cs[c], :], in0=x_tok[0:cs[c], c, :],
                                                          scalar=nstat[0:cs[c], 4 + c:5 + c], in1=gbc[0:cs[c], :],
                                                          op0=ALU.mult, op1=ALU.mult),
                      reads=xkeys(c) + [("nrstd", c), "gbc"], writes=cells(4 * c, 4 * c + 4))
            for kc in range(KC):
                b = next_bank()
                bbf = banks[b][:].bitcast(BF16)
                for c in range(nch):
                    hn = bigv(4 * c, 4)
                    S.add("pe", "transpose", dict(out=bbf[:, co[c]:co[c] + cs[c]], in_=hn[0:cs[c], kc * 128:(kc + 1) * 128],
                                                  identity=ident_bf[0:cs[c], 0:cs[c]]),
                          reads=cells(4 * c, 4 * c + 4) + ["ident_bf"], writes=[("bank", b)])
                evac_copy(hT[:, kc, 0:N], bbf[:, 0:N], reads=[("bank", b)], writes=[("hT", kc)])
            if len(segs) == 2:
                g0 = segs[1][0] - 2
                S.add("pool", "memset", dict(ap=hT[:, :, g0:g0 + 2], constant=0.0), writes=[("hT", k) for k in range(KC)])

        hT_all = [("hT", k) for k in range(KC)]

        def mm_group_fm(b, slab, slabkey, col0, N, extra_reads=()):
            for kc in range(KC):
                S.add("pe", "matmul", dict(out=banks[b][:, 0:N], lhsT=slab[:, kc, col0:col0 + 128], rhs=hT[:, kc, 0:N],
                                           start=(kc == 0), stop=(kc == KC - 1)),
                      reads=[slabkey] + hT_all + list(extra_reads), writes=[("bank", b)])

        def conv_seg(buf, bufkey, out, outkey, N, wcol, first_eng="pool"):
            S.add(first_eng, "tensor_scalar", dict(out=out[:, 0:N], in0=buf[:, 0:N], scalar1=wcol(0),
                                                   scalar2=0.0, op0=ALU.mult, op1=ALU.add),
                  reads=[bufkey], writes=[outkey])
            S.add("dve", "scalar_tensor_tensor", dict(out=out[:, 0:N], in0=buf[:, 1:N + 1], scalar=wcol(1),
                                                      in1=out[:, 0:N], op0=ALU.mult, op1=ALU.add),
                  reads=[bufkey, outkey], writes=[outkey])
            S.add("dve", "scalar_tensor_tensor", dict(out=out[:, 0:N], in0=buf[:, 2:N + 2], scalar=wcol(2),
                                                      in1=out[:, 0:N], op0=ALU.mult, op1=ALU.add),
                  reads=[bufkey, outkey], writes=[outkey])

        pend_pe = []

        def flush_pe(keep):
            while len(pend_pe) > keep:
                for (kw, reads, writes) in pend_pe.pop(0):
                    S.add("pe", "matmul", kw, reads=reads, writes=writes)

        def stat_mms(sqi, cs, co, col_base, first, last):
            ops = []
            for c in range(len(cs)):
                ops.append((dict(out=statbank[0:cs[c], col_base + c:col_base + c + 1], lhsT=sqs[sqi][:, co[c]:co[c] + cs[c]],
                                 rhs=ones_bf[:, 0:1], start=(first and c == 0 and col_base == 4), stop=last,
                                 skip_group_check=True),
                            [("sq", sqi), "ones_bf"], [("stat", col_base + c)]))
            pend_pe.append(ops)

        def mixer(tile, l, has_sample, sample_c):
            cs, co, N, segs = geom(tile)
            nch = len(cs)
            nb = (N + 127) // 128
            inkeys = conv_keys[("in", l)]
            wv = wb_in[l].rearrange("(k p) n -> p k n", p=128)
            vgbc = bigv(28, 4, F32)
            S.add("act", "dma_start", dict(out=vgbc, in_=vnorm_g[l:l + 1, :].partition_broadcast(128)),
                  writes=cells(28, 32), dsem=vgsem)
            for vs in range(2):
                slab, skey = slab_load(wv[:, :, DA + vs * 512:DA + (vs + 1) * 512], (KC, 512), inkeys, (4 * l, vs, 10))
                for c in range(nch):
                    b = next_bank()
                    for kc in range(KC):
                        S.add("pe", "matmul", dict(out=banks[b][0:cs[c], :], lhsT=hT[:, kc, co[c]:co[c] + cs[c]], rhs=slab[:, kc, :],
                                                   start=(kc == 0), stop=(kc == KC - 1)),
                              reads=[skey] + hT_all, writes=[("bank", b)])
                    tA = next_tmp()
                    tB = next_tmp()
                    n_ = cs[c]
                    idx = vs * 4 + c
                    S.add("act", "activation", dict(out=tmps[tA][0:n_, 0:512], in_=banks[b][0:n_, :], func=AF.Gelu_apprx_tanh),
                          reads=[("bank", b)], writes=[("tmp", tA)])
                    S.add("pool", "tensor_tensor", dict(out=tmps[tB][0:n_, 0:512], in0=tmps[tA][0:n_, 0:512], in1=tmps[tA][0:n_, 0:512],
                                                        op=ALU.mult),
                          reads=[("tmp", tA)], writes=[("tmp", tB)])
                    S.add("dve", "tensor_reduce", dict(out=vstat[0:n_, idx * 4:idx * 4 + 4],
                                                       in_=tmps[tB][0:n_, 0:512].rearrange("p (h d) -> p h d", h=4),
                                                       axis=AX.X, op=ALU.add),
                          reads=[("tmp", tB)], writes=[("vstat", idx)])
                    rstd_small(vstat[:, idx * 4:idx * 4 + 4], vstat[:, 32 + idx * 4:32 + idx * 4 + 4], 4, 1.0 / 128,
                               reads=[("vstat", idx)], writes=[("vrstd", idx)])
                    S.add("dve", "tensor_tensor", dict(out=tmps[tB][0:n_, 0:512].rearrange("p (h d) -> p h d", h=4),
                                                       in0=tmps[tA][0:n_, 0:512].rearrange("p (h d) -> p h d", h=4),
                                                       in1=vstat[0:n_, 32 + idx * 4:32 + idx * 4 + 4].unsqueeze(2).to_broadcast([n_, 4, 128]),
                                                       op=ALU.mult),
                          reads=[("tmp", tA), ("vrstd", idx)], writes=[("tmp", tB)])
                    vn = bigv(20 + 2 * c + vs, 1)
                    S.add("pool", "tensor_tensor", dict(out=vn[0:n_, :], in0=tmps[tB][0:n_, 0:512], in1=vgbc[0:n_, vs * 512:(vs + 1) * 512],
                                                        op=ALU.mult),
                          reads=[("tmp", tB)] + cells(28, 32), writes=cells(20 + 2 * c + vs, 21 + 2 * c + vs))
                    if has_sample and c == sample_c:
                        S.add("dve", "tensor_tensor", dict(out=tmps[tA][0:n_, 0:512], in0=tmps[tB][0:n_, 0:512],
                                                           in1=vgbc[0:n_, vs * 512:(vs + 1) * 512], op=ALU.mult),
                              reads=[("tmp", tB)] + cells(28, 32), writes=[("tmp", tA)])
                        S.add("act", "dma_start", dict(out=s_v[l, :, vs * 512:(vs + 1) * 512], in_=tmps[tA][0:n_, 0:512]),
                              reads=[("tmp", tA)], dsem=svsem)
            for jj in range(2):
                sgb = slab_load(wv[:, :, 2048 + jj * 512:2048 + (jj + 1) * 512], (KC, 512), inkeys, (4 * l, 2 + 3 * jj, 10))
                sgc = slab_load(wv[:, :, 3072 + jj * 512:3072 + (jj + 1) * 512], (KC, 512), inkeys, (4 * l, 3 + 3 * jj, 10))
                shn = slab_load(wv[:, :, 4096 + jj * 512:4096 + (jj + 1) * 512], (KC, 512), inkeys, (4 * l, 4 + 3 * jj, 10))
                for q in range(4):
                    j = jj * 4 + q
                    bgc = next_bank()
                    mm_group_fm(bgc, sgc[0], sgc[1], q * 128, N)
                    bhn = next_bank()
                    mm_group_fm(bhn, shn[0], shn[1], q * 128, N)
                    bgb = next_bank()
                    mm_group_fm(bgb, sgb[0], sgb[1], q * 128, N)
                    flush_pe(1)
                    tA = next_tmp()
                    tBuf = next_tmp()
                    tC = next_tmp()
                    sq = next_sq()
                    S.add("act", "activation", dict(out=tmps[tA][:, 0:N], in_=banks[bgc][:, 0:N], func=AF.Copy),
                          reads=[("bank", bgc)], writes=[("tmp", tA)])
                    S.add("dve", "tensor_tensor", dict(out=tmps[tBuf][:, 2:2 + N], in0=tmps[tA][:, 0:N],
                                                       in1=banks[bhn][:, 0:N], op=ALU.mult),
                          reads=[("tmp", tA), ("bank", bhn)], writes=[("tmp", tBuf)])
                    for (c0, n, sid) in segs:
                        hs = hist_s[:, l, sid, :].rearrange("p (r j) -> p r j", r=2)[:, :, j]
                        S.add("pool", "tensor_copy", dict(out=tmps[tBuf][:, c0:c0 + 2], in_=hs),
                              reads=[("hs", l, sid), "hist_s_all"], writes=[("tmp", tBuf)])
                        S.add("pool", "tensor_copy", dict(out=hs, in_=tmps[tBuf][:, c0 + n:c0 + n + 2]),
                              reads=[("tmp", tBuf)], writes=[("hs", l, sid)])
                    conv_seg(tmps[tBuf], ("tmp", tBuf), tmps[tC], ("tmp", tC), N,
                             lambda k, j=j: parA[:, l, 16 + k * 8 + j:17 + k * 8 + j])
                    S.add("dve", "tensor_tensor", dict(out=tmps[tA][:, 0:N], in0=tmps[tC][:, 0:N], in1=banks[bgb][:, 0:N], op=ALU.mult),
                          reads=[("tmp", tC), ("bank", bgb)], writes=[("tmp", tA)])
                    S.add("pool", "tensor_tensor", dict(out=sqs[sq][:, 0:N], in0=tmps[tA][:, 0:N], in1=tmps[tA][:, 0:N], op=ALU.mult),
                          reads=[("tmp", tA)], writes=[("sq", sq)])
                    S.add("dve", "tensor_scalar", dict(out=bigv(8 + j, 1)[:, 0:N], in0=tmps[tA][:, 0:N], scalar1=parA[:, l, 8 + j:9 + j],
                                                       scalar2=None, op0=ALU.mult),
                          reads=[("tmp", tA), "parA"], writes=cells(8 + j, 9 + j))
                    stat_mms(sq, cs, co, 4, j == 0, j == 7)
                    conv_tick()
            uslabs = {}
            for j in range(8):
                if j % 4 == 0:
                    uslabs[j // 4] = slab_load(wv[:, :, (j // 4) * 512:(j // 4 + 1) * 512], (KC, 512), inkeys, (4 * l, 8 + j // 4, 10))
                slab, skey = uslabs[j // 4]
                bu = next_bank()
                mm_group_fm(bu, slab, skey, (j % 4) * 128, N)
                bz = next_bank()
                if len(segs) == 1:
                    zout = banks[bz][:, 0:nb * 128].rearrange("p (c t) -> p c t", c=nb)
                    for (bt, first) in ((bhi, True), (blo, False)):
                        brow = bt[0:1, l * 1024 + j * 128:l * 1024 + (j + 1) * 128].unsqueeze(1).to_broadcast([1, nb, 128])
                        S.add("pe", "matmul", dict(out=zout, lhsT=ones_bf[0:1, :], rhs=brow, start=first, stop=False,
                                                   skip_group_check=True),
                              reads=["ones_bf", "bhi", "blo"], writes=[("bank", bz)])
                else:
                    first = True
                    for c in range(nch):
                        for bt in (bhi, blo):
                            brow = bt[0:1, l * 1024 + j * 128:l * 1024 + j * 128 + cs[c]]
                            S.add("pe", "matmul", dict(out=banks[bz][:, co[c]:co[c] + cs[c]], lhsT=ones_bf[0:1, :], rhs=brow,
                                                       start=first, stop=False, skip_group_check=True),
                                  reads=["ones_bf", "bhi", "blo"], writes=[("bank", bz)])
                            first = False
                for c in range(nch):
                    n_ = cs[c]
                    vn = bigv(20 + 2 * c + j // 4, 1)
                    S.add("pe", "matmul", dict(out=banks[bz][:, co[c]:co[c] + n_], lhsT=vn[0:n_, (j % 4) * 128:(j % 4 + 1) * 128],
                                               rhs=WsT[0:n_, l, j * 128:j * 128 + n_], start=False, stop=(c == nch - 1),
                                               skip_group_check=True),
                          reads=cells(20 + 2 * c + j // 4, 21 + 2 * c + j // 4) + ["WsT"], writes=[("bank", bz)])
                flush_pe(1)
                tA = next_tmp()
                tB = next_tmp()
                sq = next_sq()
                S.add("act", "activation", dict(out=tmps[tA][:, 0:N], in_=banks[bu][:, 0:N], func=AF.Gelu_apprx_tanh),
                      reads=[("bank", bu)], writes=[("tmp", tA)])
                S.add("dve", "tensor_tensor", dict(out=tmps[tB][:, 0:N], in0=tmps[tA][:, 0:N], in1=banks[bz][:, 0:N], op=ALU.mult),
                      reads=[("tmp", tA), ("bank", bz)], writes=[("tmp", tB)])
                S.add("pool", "tensor_tensor", dict(out=sqs[sq][:, 0:N], in0=tmps[tB][:, 0:N], in1=tmps[tB][:, 0:N], op=ALU.mult),
                      reads=[("tmp", tB)], writes=[("sq", sq)])
                S.add("dve", "tensor_scalar", dict(out=bigv(j, 1)[:, 0:N], in0=tmps[tB][:, 0:N], scalar1=parA[:, l, j:j + 1],
                                                   scalar2=None, op0=ALU.mult),
                      reads=[("tmp", tB), "parA"], writes=cells(j, j + 1))
                stat_mms(sq, cs, co, 0, j == 0, j == 7)
                conv_tick()
            flush_pe(0)
            S.add("dve", "tensor_copy", dict(out=rstat[:, 0:8], in_=statbank[:, 0:8]), reads=[("stat", i) for i in range(8)],
                  writes=["rs_raw"])
            rstd_small(rstat[:, 0:8], rstat[:, 8:16], 8, 1.0 / 1024, reads=["rs_raw"], writes=["rs"])

        def wout(tile, l):
            cs, co, N, segs = geom(tile)
            nch = len(cs)
            wv = wb_out[l].rearrange("(k p) n -> p k n", p=128)
            for ng in range(4):
                slab, skey = slab_load(wv[:, :, ng * 512:(ng + 1) * 512], (KC, 512), conv_keys[("out", l)], (4 * l + 1, ng, 4))
                for c in range(nch):
                    n_ = cs[c]
                    for half in (1, 0):
                        b = next_bank()
                        for i in range(8):
                            kc = half * 8 + i
                            S.add("pe", "matmul", dict(out=banks[b][0:n_, :], lhsT=bigv(kc, 1)[:, co[c]:co[c] + n_], rhs=slab[:, kc, :],
                                                       start=(i == 0), stop=(i == 7)),
                                  reads=[skey] + cells(kc, kc + 1), writes=[("bank", b)])
                        xs_ = x_tok[0:n_, c, ng * 512:(ng + 1) * 512]
                        S.add("dve", "scalar_tensor_tensor", dict(out=xs_, in0=banks[b][0:n_, :],
                                                                  scalar=rstat[0:n_, 8 + half * 4 + c:9 + half * 4 + c],
                                                                  in1=xs_, op0=ALU.mult, op1=ALU.add),
                              reads=[("bank", b), "rs", ("x", c, ng)], writes=[("x", c, ng)])

        def ffn(tile, l):
            cs, co, N, segs = geom(tile)
            nch = len(cs)
            wv = wb_up[l].rearrange("(k p) n -> p k n", p=128)
            upkeys = conv_keys[("up", l)]
            pendC = []

            def stageC():
                while pendC:
                    tG, tCG, tCV, j = pendC.pop(0)
                    S.add("act", "activation", dict(out=tmps[tG][:, 0:N], in_=tmps[tCG][:, 0:N], func=AF.Silu),
                          reads=[("tmp", tCG)], writes=[("tmp", tG)])
                    S.add("pool", "tensor_tensor", dict(out=bigv(j, 1)[:, 0:N], in0=tmps[tG][:, 0:N], in1=tmps[tCV][:, 0:N], op=ALU.mult),
                          reads=[("tmp", tG), ("tmp", tCV)], writes=cells(j, j + 1))

            for pg in range(11):
                sg_ = slab_load(wv[:, :, pg * 512:(pg + 1) * 512], (KC, 512), upkeys, (4 * l + 2, 2 * pg, 22))
                sv_ = slab_load(wv[:, :, DFF + pg * 512:DFF + (pg + 1) * 512], (KC, 512), upkeys, (4 * l + 2, 2 * pg + 1, 22))
                for q in range(4):
                    j = pg * 4 + q
                    bg = next_bank()
                    mm_group_fm(bg, sg_[0], sg_[1], q * 128, N)
                    bv = next_bank()
                    mm_group_fm(bv, sv_[0], sv_[1], q * 128, N)
                    tG = next_tmp()
                    tV = next_tmp()
                    tCG = next_tmp()
                    tCV = next_tmp()
                    for (bk, tb, ch) in ((bg, tG, j), (bv, tV, FC + j)):
                        S.add("act", "activation", dict(out=tmps[tb][:, 2:2 + N], in_=banks[bk][:, 0:N], func=AF.Copy),
                              reads=[("bank", bk)], writes=[("tmp", tb)])
                        for (c0, n, sid) in segs:
                            hf = hist_f[:, l, sid, :].rearrange("p (r j) -> p r j", r=2)[:, :, ch]
                            S.add("pool", "tensor_copy", dict(out=tmps[tb][:, c0:c0 + 2], in_=hf),
                                  reads=[("hf", l, sid, ch), "hist_f_all", ("hfinit", l, 0), ("hfinit", l, 1)],
                                  writes=[("tmp", tb)])
                            S.add("pool", "tensor_copy", dict(out=hf, in_=tmps[tb][:, c0 + n:c0 + n + 2]),
                                  reads=[("tmp", tb)], writes=[("hf", l, sid, ch)])
                    conv_seg(tmps[tG], ("tmp", tG), tmps[tCG], ("tmp", tCG), N, lambda k, ch=j: cwf[:, l, k, ch:ch + 1])
                    conv_seg(tmps[tV], ("tmp", tV), tmps[tCV], ("tmp", tCV), N, lambda k, ch=FC + j: cwf[:, l, k, ch:ch + 1])
                    conv_tick()
                    stageC()
                    pendC.append((tG, tCG, tCV, j))
            stageC()
            wd = wb_down[l].rearrange("(k p) n -> p k n", p=128)
            dkeys = conv_keys[("down", l)]
            for ng in range(4):
                bks = [next_bank() for _ in range(nch)]
                for kq in range(4):
                    slab, skey = slab_load(wd[:, kq * 11:(kq + 1) * 11, ng * 512:(ng + 1) * 512], (11, 512), dkeys, (4 * l + 3, ng * 4 + kq, 16))
                    for c in range(nch):
                        n_ = cs[c]
                        for i in range(11):
                            kc = kq * 11 + i
                            S.add("pe", "matmul", dict(out=banks[bks[c]][0:n_, :], lhsT=bigv(kc, 1)[:, co[c]:co[c] + n_], rhs=slab[:, i, :],
                                                       start=(kq == 0 and i == 0), stop=(kq == 3 and i == 10)),
                                  reads=[skey] + cells(kc, kc + 1), writes=[("bank", bks[c])])
                for c in range(nch):
                    n_ = cs[c]
                    xs_ = x_tok[0:n_, c, ng * 512:(ng + 1) * 512]
                    S.add("dve", "tensor_tensor", dict(out=xs_, in0=xs_, in1=banks[bks[c]][0:n_, :], op=ALU.add),
                          reads=[("bank", bks[c]), ("x", c, ng)], writes=[("x", c, ng)])

        def final_out(tile):
            cs, co, N, segs = geom(tile)
            nch = len(cs)
            load_gbc(final_g[0:1, :])
            junk = bigv(16, 4)
            for c in range(nch):
                S.add("act", "activation", dict(out=junk[0:cs[c], :], in_=x_tok[0:cs[c], c, :], func=AF.Square,
                                                accum_out=nstat[0:cs[c], c:c + 1]),
                      reads=xkeys(c), writes=cells(16, 20) + [("nstat", c)])
                rstd_small(nstat[:, c:c + 1], nstat[:, 4 + c:5 + c], 1, 1.0 / D,
                           reads=[("nstat", c)], writes=[("nrstd", c)])
                S.add("dve", "scalar_tensor_tensor", dict(out=x_tok[0:cs[c], c, :], in0=x_tok[0:cs[c], c, :],
                                                          scalar=nstat[0:cs[c], 4 + c:5 + c], in1=gbc[0:cs[c], :],
                                                          op0=ALU.mult, op1=ALU.mult),
                      reads=xkeys(c) + [("nrstd", c), "gbc"], writes=xkeys(c))
            for c, (kind, ci, n_) in enumerate(tile):
                if kind == "p":
                    if ci < HALO:
                        continue
                    dst = yp[(ci - HALO) * 128:(ci - HALO + 1) * 128, :]
                else:
                    dst = ys[:, :]
                S.add("act", "dma_start", dict(out=dst, in_=x_tok[0:n_, c, :]), reads=xkeys(c), dsem=ysems[c])

        for ti, tile in enumerate(tiles):
            has_sample = any(k == "s" for (k, _, _) in tile)
            sample_c = [i for i, t in enumerate(tile) if t[0] == "s"]
            sample_c = sample_c[0] if sample_c else -1
            for c, (kind, ci, n_) in enumerate(tile):
                if ti == 0:
                    break
                src = xp[ci * 128:(ci + 1) * 128, :] if kind == "p" else xs[:, :]
                S.add("act", "dma_start", dict(out=x_tok[0:n_, c, :], in_=src), writes=xkeys(c), dsem=xsems[c])
            for l in range(L):
                if l == 1:
                    conv_tick(flush=True)
                norm_to_hT(tile, n1_g[l:l + 1, :])
                mixer(tile, l, has_sample, sample_c)
                wout(tile, l)
                norm_to_hT(tile, n2_g[l:l + 1, :])
                ffn(tile, l)
            final_out(tile)

        for l in range(L):
            for sid, (o_s, o_f) in enumerate(((p_sconv, p_ffn), (s_sconv, s_ffn))):
                b = next_bank()
                S.add("pe", "transpose", dict(out=banks[b][0:16, 0:128], in_=hist_s[:, l, sid, :], identity=ident_f[:, :]),
                      reads=[("hs", l, sid), "hist_s_all", "ident_f"], writes=[("bank", b)])
                S.add("dve", "tensor_copy", dict(out=outst[0:16, :], in_=banks[b][0:16, 0:128]), reads=[("bank", b)], writes=["outst"])
                S.add("act", "dma_start", dict(out=o_s[l].rearrange("r (j p) -> (r j) p", p=128), in_=outst[0:16, :]),
                      reads=["outst"], dsem=stsem)
                for r in range(2):
                    b = next_bank()
                    S.add("pe", "transpose", dict(out=banks[b][0:88, 0:128], in_=hist_f[:, l, sid, r * 88:(r + 1) * 88], identity=ident_f[:, :]),
                          reads=[("hf", l, sid, ch) for ch in range(88)] + [("hfinit", l, 0), ("hfinit", l, 1), "hist_f_all", "ident_f"],
                          writes=[("bank", b)])
                    S.add("dve", "tensor_copy", dict(out=outst[0:88, :], in_=banks[b][0:88, 0:128]), reads=[("bank", b)], writes=["outst"])
                    S.add("act", "dma_start", dict(out=o_f[l, r].rearrange("(j p) -> j p", p=128), in_=outst[0:88, :]),
                          reads=["outst"], dsem=stsem)

        fw = [(d.handle, d.count) for d in sem_pool if d.count > 0]
        S.emit(block, esem, final_waits=fw)
        nc._sched_stats = S.stats()
        nc._sched = S
    return nc


_CACHE = {}


def _get_nc(npc):
    if npc not in _CACHE:
        _CACHE[npc] = build(npc)
    return _CACHE[npc]


def run_cores(inputs, seq, npc_own):
    f = lambda a: np.ascontiguousarray(np.asarray(a, dtype=np.float32))
    x_prompt = f(inputs["x_prompt"])
    x_sample = f(inputs["x_sample"])
    state_sconv = f(inputs["state_sconv"])
    state_ffnconv = f(inputs["state_ffnconv"])
    nb = x_prompt.shape[0]
    per = NCORES // nb
    own = seq // per
    assert own == npc_own * 128
    npc = npc_own + HALO
    nc = _get_nc(npc)
    shared = {
        "n1_g": f(inputs["n1_g"]), "w_in": f(inputs["w_in"]), "vnorm_g": f(inputs["vnorm_g"]),
        "gmlp_ws": f(inputs["gmlp_ws"]), "gmlp_bs": f(inputs["gmlp_bs"]).reshape(1, L * 1024),
        "sconv_w": f(inputs["sconv_w"]), "onorm_a_g": f(inputs["onorm_a_g"]), "onorm_b_g": f(inputs["onorm_b_g"]),
        "w_out": f(inputs["w_out"]), "n2_g": f(inputs["n2_g"]), "ffn_up": f(inputs["ffn_up"]),
        "ffn_conv_w": f(inputs["ffn_conv_w"]), "ffn_down": f(inputs["ffn_down"]),
        "final_g": f(inputs["final_g"]).reshape(1, D),
    }
    in_maps = []
    for i in range(NCORES):
        b, q = divmod(i, per)
        xp = np.zeros((npc * 128, D), np.float32)
        lo = q * own - HALO * 128
        if lo >= 0:
            xp[:] = x_prompt[b, lo:(q + 1) * own]
        else:
            xp[HALO * 128:] = x_prompt[b, 0:own]
        m = dict(shared)
        m["xp"] = xp
        m["xs"] = x_sample[i]
        m["st_s"] = np.ascontiguousarray(state_sconv[:, i])
        m["st_f"] = np.ascontiguousarray(state_ffnconv[:, i])
        in_maps.append(m)
    res = run_bass_kernel_spmd(nc, in_maps, core_ids=list(range(NCORES)))
    R = res.results
    y_prompt = np.empty((nb, seq, D), np.float32)
    y_sample = np.empty((NCORES, 64, D), np.float32)
    p_sconv = np.empty((L, nb, 2, DB), np.float32)
    p_ffn = np.empty((L, nb, 2, FUP), np.float32)
    s_v = np.empty((L, NCORES, 64, DA), np.float32)
    s_sconv = np.empty((L, NCORES, 2, DB), np.float32)
    s_ffn = np.empty((L, NCORES, 2, FUP), np.float32)
    for i in range(NCORES):
        b, q = divmod(i, per)
        r = R[i]
        y_prompt[b, q * own:(q + 1) * own] = r["yp"]
        y_sample[i] = r["ys"]
        s_v[:, i] = r["s_v"]
        s_sconv[:, i] = r["s_sconv"]
        s_ffn[:, i] = r["s_ffn"]
        if q == per - 1:
            p_sconv[:, b] = r["p_sconv"]
            p_ffn[:, b] = r["p_ffn"]
    return (y_prompt, y_sample, p_sconv, p_ffn, s_v, s_sconv, s_ffn)


def kernel(**inputs):
    return run_cores(inputs, SEQ, SEQ // (NCORES // 2) // 128)
```
